# Optimizing a Trainium2 kernel written in Bass

```python
import math
import jax, jax.numpy as jnp
from jax import lax
import numpy as np

D_MODEL = 2048
BATCH = 4
SEQ = 4096
DEPTH = 2

N_A_LAYERS = DEPTH // 2
N_B_LAYERS = DEPTH - N_A_LAYERS
CONV_WIDTH = 3
D_FF = ((8 * D_MODEL // 3 + 255) // 256) * 256
DIFF_HEADS = D_MODEL // 256
DIFF_HEAD_DIM = D_MODEL // DIFF_HEADS // 2
V_HEAD_DIM = 2 * DIFF_HEAD_DIM
QK_WIDTH = DIFF_HEADS * 2 * DIFF_HEAD_DIM
V_WIDTH = DIFF_HEADS * V_HEAD_DIM
ROT_DIM = DIFF_HEAD_DIM // 4
ROPE_THETA = 500000.0
Q_BLOCK = 128
DEEPNORM_ALPHA = (2.0 * DEPTH) ** 0.25
DEEPNORM_BETA = (8.0 * DEPTH) ** -0.25
LN_EPS = 1e-5
SUBLN_EPS = 1e-5

kernel_name = 'hybrid_shortconv_yoco_diffattn_convffn_deepnorm'


def causal_dwconv(x, w):
    s = x.shape[1]
    xp = jnp.pad(x, ((0, 0), (CONV_WIDTH - 1, 0), (0, 0)))
    y = xp[:, 0:s] * w[0]
    for j in range(1, CONV_WIDTH):
        y = y + xp[:, j:j + s] * w[j]
    return y


def layer_norm(x, g, b):
    xf = x.astype(jnp.float32)
    mu = jnp.mean(xf, axis=-1, keepdims=True)
    var = jnp.mean(jnp.square(xf - mu), axis=-1, keepdims=True)
    y = (xf - mu) * lax.rsqrt(var + LN_EPS)
    return (y * g.astype(jnp.float32) + b.astype(jnp.float32)).astype(x.dtype)


def rms_norm(x, g):
    xf = x.astype(jnp.float32)
    y = xf * lax.rsqrt(jnp.mean(jnp.square(xf), axis=-1, keepdims=True) + SUBLN_EPS)
    return (y * g.astype(jnp.float32)).astype(x.dtype)


def rope_tables(positions):
    inv_freq = ROPE_THETA ** (-jnp.arange(0, ROT_DIM, 2, dtype=jnp.float32) / ROT_DIM)
    ang = positions.astype(jnp.float32)[..., None] * inv_freq
    return jnp.cos(ang), jnp.sin(ang)


def partial_rope(t, cos, sin):
    half = ROT_DIM // 2
    rot = t[..., :ROT_DIM].astype(jnp.float32)
    x1, x2 = rot[..., :half], rot[..., half:]
    c = cos[:, :, None, None, :]
    s = sin[:, :, None, None, :]
    rotated = jnp.concatenate([x1 * c - x2 * s, x2 * c + x1 * s], axis=-1).astype(t.dtype)
    return jnp.concatenate([rotated, t[..., ROT_DIM:]], axis=-1)


def short_conv_mixer(x, w_in, conv_w, w_out):
    b_gate, c_gate, xv = jnp.split(x @ w_in, 3, axis=-1)
    y = b_gate * causal_dwconv(c_gate * xv, conv_w)
    return y @ w_out


def conv_ffn(x, w_up, conv_w, conv_b, w_down):
    h = causal_dwconv(x @ w_up, conv_w) + conv_b
    g, u = jnp.split(h, 2, axis=-1)
    return (jax.nn.silu(g) * u) @ w_down


def shared_kv(x, w_k, w_v, cos, sin):
    bsz, s, _ = x.shape
    k = partial_rope((x @ w_k).reshape(bsz, s, DIFF_HEADS, 2, DIFF_HEAD_DIM), cos, sin)
    v = (x @ w_v).reshape(bsz, s, DIFF_HEADS, V_HEAD_DIM)
    return k, v


def diff_attention(x, k, v, cos, sin, w_q, lam, subln_g, w_o, lambda_init):
    bsz, s, _ = x.shape
    q = (x @ w_q).reshape(bsz, s, DIFF_HEADS, 2, DIFF_HEAD_DIM)
    q = partial_rope(q, cos, sin) * (DIFF_HEAD_DIM ** -0.5)
    lamf = lam.astype(jnp.float32)
    lam_full = (jnp.exp(jnp.sum(lamf[0] * lamf[1])) - jnp.exp(jnp.sum(lamf[2] * lamf[3]))
                + lambda_init)
    kpos = jnp.arange(s)
    neg = jnp.finfo(jnp.float32).min

    def block(i):
        start = i * Q_BLOCK
        qb = lax.dynamic_slice_in_dim(q, start, Q_BLOCK, axis=1)
        sc = jnp.einsum('bqhcd,bkhcd->bhcqk', qb, k).astype(jnp.float32)
        qpos = start + jnp.arange(Q_BLOCK)
        mask = kpos[None, :] <= qpos[:, None]
        p = jax.nn.softmax(jnp.where(mask, sc, neg), axis=-1)
        a = p[:, :, 0] - lam_full * p[:, :, 1]
        return jnp.einsum('bhqk,bkhe->bqhe', a.astype(v.dtype), v)

    o = lax.map(block, jnp.arange(s // Q_BLOCK))
    o = jnp.moveaxis(o, 0, 1).reshape(bsz, s, DIFF_HEADS, V_HEAD_DIM)
    o = rms_norm(o, subln_g) * (1.0 - lambda_init)
    return o.reshape(bsz, s, V_WIDTH) @ w_o


def setup_inputs(seed: int = 0) -> dict:
    key = jax.random.key(seed)
    ks = jax.random.split(key, 16)
    f32 = jnp.float32
    nrm = lambda k, shape: jax.random.normal(k, shape, dtype=f32)
    x = nrm(ks[0], (BATCH, SEQ, D_MODEL))
    positions = jnp.broadcast_to(jnp.arange(SEQ, dtype=jnp.int32), (BATCH, SEQ))
    ln_g = 1.0 + 0.02 * nrm(ks[1], (DEPTH, 2, D_MODEL))
    ln_b = 0.02 * nrm(ks[2], (DEPTH, 2, D_MODEL))
    a_w_in = nrm(ks[3], (N_A_LAYERS, D_MODEL, 3 * D_MODEL)) * D_MODEL ** -0.5
    a_conv_w = nrm(ks[4], (N_A_LAYERS, CONV_WIDTH, D_MODEL)) * CONV_WIDTH ** -0.5
    a_w_out = nrm(ks[5], (N_A_LAYERS, D_MODEL, D_MODEL)) * (D_MODEL ** -0.5 * DEEPNORM_BETA)
    kv_w_k = nrm(ks[6], (D_MODEL, QK_WIDTH)) * D_MODEL ** -0.5
    kv_w_v = nrm(ks[7], (D_MODEL, V_WIDTH)) * (D_MODEL ** -0.5 * DEEPNORM_BETA)
    b_w_q = nrm(ks[8], (N_B_LAYERS, D_MODEL, QK_WIDTH)) * D_MODEL ** -0.5
    b_lambda = 0.1 * nrm(ks[9], (N_B_LAYERS, 4, DIFF_HEAD_DIM))
    b_subln_g = 1.0 + 0.02 * nrm(ks[10], (N_B_LAYERS, V_HEAD_DIM))
    b_w_o = nrm(ks[11], (N_B_LAYERS, V_WIDTH, D_MODEL)) * (V_WIDTH ** -0.5 * DEEPNORM_BETA)
    ffn_w_up = nrm(ks[12], (DEPTH, D_MODEL, 2 * D_FF)) * D_MODEL ** -0.5
    ffn_conv_w = nrm(ks[13], (DEPTH, CONV_WIDTH, 2 * D_FF)) * CONV_WIDTH ** -0.5
    ffn_conv_b = 0.02 * nrm(ks[14], (DEPTH, 2 * D_FF))
    ffn_w_down = nrm(ks[15], (DEPTH, D_FF, D_MODEL)) * (D_FF ** -0.5 * DEEPNORM_BETA)
    return {'x': x, 'positions': positions, 'ln_g': ln_g, 'ln_b': ln_b,
            'a_w_in': a_w_in, 'a_conv_w': a_conv_w, 'a_w_out': a_w_out,
            'kv_w_k': kv_w_k, 'kv_w_v': kv_w_v,
            'b_w_q': b_w_q, 'b_lambda': b_lambda, 'b_subln_g': b_subln_g, 'b_w_o': b_w_o,
            'ffn_w_up': ffn_w_up, 'ffn_conv_w': ffn_conv_w, 'ffn_conv_b': ffn_conv_b,
            'ffn_w_down': ffn_w_down}


def reference(x, positions, ln_g, ln_b, a_w_in, a_conv_w, a_w_out, kv_w_k, kv_w_v,
              b_w_q, b_lambda, b_subln_g, b_w_o, ffn_w_up, ffn_conv_w, ffn_conv_b,
              ffn_w_down):
    cos, sin = rope_tables(positions)
    k_sh, v_sh = None, None
    for layer in range(DEPTH):
        if layer < N_A_LAYERS:
            mix = short_conv_mixer(x, a_w_in[layer], a_conv_w[layer], a_w_out[layer])
        else:
            j = layer - N_A_LAYERS
            if j == 0:
                k_sh, v_sh = shared_kv(x, kv_w_k, kv_w_v, cos, sin)
            lambda_init = 0.8 - 0.6 * math.exp(-0.3 * layer)
            mix = diff_attention(x, k_sh, v_sh, cos, sin, b_w_q[j], b_lambda[j],
                                 b_subln_g[j], b_w_o[j], lambda_init)
        x = layer_norm(DEEPNORM_ALPHA * x + mix, ln_g[layer, 0], ln_b[layer, 0])
        ffn = conv_ffn(x, ffn_w_up[layer], ffn_conv_w[layer], ffn_conv_b[layer], ffn_w_down[layer])
        x = layer_norm(DEEPNORM_ALPHA * x + ffn, ln_g[layer, 1], ln_b[layer, 1])
    return x
```

```python
import math
from contextlib import ExitStack

import numpy as np
import ml_dtypes
import concourse.bass as bass
import concourse.mybir as mybir
from concourse.bass_utils import run_bass_kernel_spmd

F32 = mybir.dt.float32
BF16 = mybir.dt.bfloat16
I32 = mybir.dt.int32
AF = mybir.ActivationFunctionType
ALU = mybir.AluOpType
NPBF = ml_dtypes.bfloat16

D = 2048
B = 4
S = 4096
DFF = 5632
NCORE = 8
TOK = B * S // NCORE
ALPHA = (2.0 * 2) ** 0.25
LN_EPS = 1e-5
LAMBDA_INIT = 0.8 - 0.6 * math.exp(-0.3 * 1)
ROPE_THETA = 500000.0
SEM_CAP = 30000


class Tile:
    __slots__ = ("ap", "w", "r")

    def __init__(self, ap=None):
        self.ap = ap
        self.w = None
        self.r = {}


class Op:
    __slots__ = ("eng", "fn", "reads", "writes", "dma", "key", "signal", "waits", "sem", "val",
                 "inc")


ENGS = ("pe", "act", "dve", "pool", "sp")


class Fused:
    def __init__(self):
        self.nc = bass.Bass("TRN2", target_bir_lowering=False)
        self.gs = ExitStack()
        self.sems = {}
        self.ops = []
        self.nd = 0

    def dram(self, name, shape, dt, kind="Internal"):
        return self.nc.dram_tensor(name, list(shape), dt, kind=kind).ap()

    def sem(self, sk):
        if sk not in self.sems:
            self.sems[sk] = self.gs.enter_context(
                self.nc.semaphore("s_%s_%d" % (sk[0], sk[1])))
        return self.sems[sk]

    def finish(self):
        ops = self.ops
        counts = {}
        seen = {e: {} for e in ENGS}
        deps_all = []
        for k, o in enumerate(ops):
            deps = {}
            if o.eng is None:
                deps_all.append(deps)
                continue

            def need(idx):
                if idx is None:
                    return
                d = ops[idx]
                if (not d.dma) and (not o.dma) and d.eng == "pe" and o.eng == "pe":
                    return
                if deps.get(d.key, -1) < idx:
                    deps[d.key] = idx

            for t in o.reads:
                need(t.w)
            for t in o.writes:
                if not (t.w is not None and o.dma and ops[t.w].dma and ops[t.w].key == o.key):
                    need(t.w)
                for idx in t.r.values():
                    need(idx)
            for t in o.reads:
                if t.r.get(o.key, -1) < k:
                    t.r[o.key] = k
            for t in o.writes:
                t.w = k
                t.r = {}
            for idx in deps.values():
                ops[idx].signal = True
            deps_all.append(deps)
        slot = {}
        last = {}
        nbar = 0
        for o in ops:
            if o.eng is None:
                nbar += 1
                o.val = nbar
                o.waits = list(last.values())
                slot = {}
                continue
            if not o.signal:
                continue
            if isinstance(o.key, tuple):
                if o.key not in slot:
                    slot[o.key] = "%s%d" % (o.key[0], len(slot))
                name, cap = slot[o.key], 1800
            else:
                name, cap = o.key, SEM_CAP
            c = counts.get(name, 0) + 1
            counts[name] = c
            o.sem = (name, (c - 1) // cap)
            o.val = ((c - 1) % cap + 1) * o.inc
            if o.dma:
                last[name] = (o.sem, o.val)
        for k, o in enumerate(ops):
            if o.eng is None:
                continue
            sn = seen[o.eng]
            for idx in deps_all[k].values():
                d = ops[idx]
                if sn.get(d.sem, 0) < d.val:
                    sn[d.sem] = d.val
                    o.waits.append((d.sem, d.val))
        for o in ops:
            if o.eng is not None and o.signal:
                self.sem(o.sem)
        for en in ENGS:
            self.sem(("bar_" + en, 0))
        nc = self.nc

        def emit(engname, e):
            for o in ops:
                if o.eng is None:
                    if engname == "sp":
                        for (sk, v) in o.waits:
                            e.wait_ge(self.sem(sk), v)
                    e.drain().then_inc(self.sem(("bar_" + engname, 0)), 1)
                    for other in ENGS:
                        if other != engname:
                            e.wait_ge(self.sem(("bar_" + other, 0)), o.val)
                    continue
                if o.eng != engname:
                    continue
                for (sk, v) in o.waits:
                    e.wait_ge(self.sem(sk), v)
                ins = o.fn(e)
                if o.signal:
                    ins.then_inc(self.sem(o.sem), o.inc)

        with nc.Block() as block:
            @block.tensor
            def _(e):
                emit("pe", e)

            @block.scalar
            def _(e):
                emit("act", e)

            @block.vector
            def _(e):
                emit("dve", e)

            @block.gpsimd
            def _(e):
                emit("pool", e)

            @block.sync
            def _(e):
                emit("sp", e)
        self.gs.close()
        return self.nc


class Prog:
    def __init__(self, fz):
        self.fz = fz
        self.nc = fz.nc
        self.es = ExitStack()
        self.ops = fz.ops

    def sbuf(self, shape, dt, name=None):
        self.fz.nd += 1
        return self.es.enter_context(self.nc.sbuf_tensor(name or "sb%d" % self.fz.nd, list(shape), dt))

    def psum(self, shape=(128, 512), dt=F32, name=None):
        self.fz.nd += 1
        return self.es.enter_context(self.nc.psum_tensor(name or "ps%d" % self.fz.nd, list(shape), dt))

    def op(self, eng, fn, reads=(), writes=()):
        o = Op()
        o.eng, o.fn, o.reads, o.writes = eng, fn, tuple(reads), tuple(writes)
        o.dma, o.key, o.signal, o.waits, o.inc = False, eng, False, [], 1
        self.ops.append(o)
        return o

    def dma(self, q, out, in_, reads=(), writes=(), stream=None, **kw):
        o = Op()
        o.eng = q
        o.fn = lambda e: e.dma_start(out=out, in_=in_, **kw)
        o.reads, o.writes = tuple(reads), tuple(writes)
        anchor = o.writes[0] if o.writes else o.reads[0]
        o.dma, o.key, o.signal, o.waits, o.inc = True, ("dma", id(anchor)), True, [], 16
        self.ops.append(o)
        return o

    def coll(self, kind, groups, in_ap, out_ap, reads=(), writes=()):
        o = Op()
        o.eng = "pool"
        o.fn = lambda e: e.collective_compute(kind, ALU.bypass, replica_groups=groups,
                                              ins=[in_ap], outs=[out_ap])
        o.reads, o.writes = tuple(reads), tuple(writes)
        o.dma, o.key, o.signal, o.waits, o.inc = True, ("cc", id(o)), True, [], 1
        self.ops.append(o)
        return o

    def build(self):
        o = Op()
        o.eng, o.fn, o.reads, o.writes = None, None, (), ()
        o.dma, o.key, o.signal, o.waits, o.inc = False, None, False, [], 1
        self.ops.append(o)
        self.es.close()


def tiles_of(n, tw):
    out = []
    c = 0
    while c < n:
        w = min(tw, n - c)
        out.append((c, w))
        c += w
    return out


class Banks:
    def __init__(self, P, n):
        self.b = [P.psum() for _ in range(n)]
        self.t = [Tile() for _ in range(n)]
        self.i = 0

    def next(self):
        k = self.i % len(self.b)
        self.i += 1
        return self.b[k], self.t[k]


def load_weight_chunk(P, wbuf, wt, src, kc, per):
    step = max(1, 2048 // per)
    i = 0
    while i < kc:
        n = min(step, kc - i)
        P.dma("pool", wbuf[:, i * per:(i + n) * per], src[:, i * per:(i + n) * per],
              writes=[wt], stream="w")
        i += n


def emit_mixin(fz, xT, wr, cw, yT, WI):
    P = Prog(fz)
    WO = WI - 2
    X = P.sbuf([128, 16, WI], BF16)
    Xt = Tile()
    cws = P.sbuf([128, 16 * 3], F32)
    cwt = Tile()
    wb = [P.sbuf([128, 16 * 384], BF16) for _ in range(2)]
    wbt = [Tile(), Tile()]
    vrow = [P.sbuf([128, WI], F32) for _ in range(2)]
    brow = [P.sbuf([128, WI], F32) for _ in range(2)]
    urow = [P.sbuf([128, WI], F32) for _ in range(2)]
    vt = [Tile(), Tile()]
    bt = [Tile(), Tile()]
    ut = [Tile(), Tile()]
    tmp = P.sbuf([128, WO], F32)
    tmpt = Tile()
    yrow = [P.sbuf([128, WO], BF16) for _ in range(2)]
    yt = [Tile(), Tile()]
    banks = Banks(P, 6)

    P.dma("sp", cws[:, :], cw[:, :], writes=[cwt])
    half = WI // 2
    for i in range(16):
        for (c0, n) in ((0, half), (half, WI - half)):
            P.dma("pool", X[:, i, c0:c0 + n], xT[i * 128:(i + 1) * 128, c0:c0 + n], writes=[Xt])
    tl = tiles_of(WI, 411)
    load_weight_chunk(P, wb[0], wbt[0], wr[0], 16, 384)
    for j in range(16):
        s = j % 2
        if j + 1 < 16:
            load_weight_chunk(P, wb[1 - s], wbt[1 - s], wr[j + 1], 16, 384)
        for (c0, n) in tl:
            pss = [banks.next() for _ in range(3)]
            for m in range(3):
                ps, pt = pss[m]
                for i in range(16):
                    P.op("pe", lambda e, ps=ps, s=s, i=i, m=m, c0=c0, n=n: e.matmul(
                        ps[:, 0:n], wb[s][:, i * 384 + m * 128:i * 384 + (m + 1) * 128],
                        X[:, i, c0:c0 + n], start=(i == 0), stop=(i == 15)),
                        reads=[wbt[s], Xt], writes=[pt])
            (pb, pbt), (pc, pct), (pv, pvt) = pss
            P.op("act", lambda e, pv=pv, s=s, c0=c0, n=n: e.activation(
                out=vrow[s][:, c0:c0 + n], in_=pv[:, 0:n], func=AF.Copy),
                reads=[pvt], writes=[vt[s]])
            P.op("act", lambda e, pb=pb, s=s, c0=c0, n=n: e.activation(
                out=brow[s][:, c0:c0 + n], in_=pb[:, 0:n], func=AF.Copy),
                reads=[pbt], writes=[bt[s]])
            P.op("dve", lambda e, pc=pc, s=s, c0=c0, n=n: e.tensor_tensor(
                out=urow[s][:, c0:c0 + n], in0=pc[:, 0:n], in1=vrow[s][:, c0:c0 + n], op=ALU.mult),
                reads=[pct, vt[s]], writes=[ut[s]])
        P.op("dve", lambda e, s=s, j=j: e.tensor_scalar(
            out=tmp[:, :], in0=urow[s][:, 0:WO], scalar1=cws[:, 3 * j:3 * j + 1], scalar2=None,
            op0=ALU.mult), reads=[ut[s], cwt], writes=[tmpt])
        for k in (1, 2):
            P.op("dve", lambda e, s=s, j=j, k=k: e.scalar_tensor_tensor(
                out=tmp[:, :], in0=urow[s][:, k:k + WO], scalar=cws[:, 3 * j + k:3 * j + k + 1],
                in1=tmp[:, :], op0=ALU.mult, op1=ALU.add), reads=[ut[s], cwt, tmpt], writes=[tmpt])
        P.op("dve", lambda e, s=s: e.tensor_tensor(
            out=yrow[s][:, :], in0=tmp[:, :], in1=brow[s][:, 2:2 + WO], op=ALU.mult),
            reads=[tmpt, bt[s]], writes=[yt[s]])
        P.dma("sp", yT[j * 128:(j + 1) * 128, :], yrow[s][:, :], reads=[yt[s]])
    P.build()


def emit_pln(fz, xsrc, wr, resid, gb, Z, out, KC, NC, TW, xsel=None):
    P = Prog(fz)
    Zt = Tile()
    X = P.sbuf([128, KC, NC], BF16)
    Xt = Tile()
    gbs = P.sbuf([128, 32], F32)
    gbt = Tile()
    wb = [P.sbuf([128, KC * 128], BF16) for _ in range(2)]
    wbt = [Tile(), Tile()]
    rrow = [P.sbuf([128, NC], F32) for _ in range(2)]
    rt = [Tile(), Tile()]
    zrow = [P.sbuf([128, NC], F32) for _ in range(2)]
    zt = [Tile(), Tile()]
    orow = [P.sbuf([128, NC], F32) for _ in range(2)]
    ot = [Tile(), Tile()]
    sq = [P.sbuf([128, 512], F32) for _ in range(2)]
    sqt = [Tile(), Tile()]
    s1 = P.sbuf([128, NC], F32)
    s2 = P.sbuf([128, NC], F32)
    tl = tiles_of(NC, TW)
    s1t = [Tile() for _ in tl]
    s2t = [Tile() for _ in tl]
    mean = P.sbuf([128, NC], F32)
    rstd = P.sbuf([128, NC], F32)
    nb = P.sbuf([128, NC], F32)
    stt = [Tile() for _ in tl]
    ones = P.sbuf([128, 128], F32)
    onest = Tile()
    banks = Banks(P, 6)

    P.op("dve", lambda e: e.memset(ones[:, :], 1.0), writes=[onest])
    P.op("dve", lambda e: e.memset(s1[:, :], 0.0), writes=s1t)
    P.op("dve", lambda e: e.memset(s2[:, :], 0.0), writes=s2t)
    P.dma("sp", gbs[:, :], gb[:, :], writes=[gbt])
    if xsel is None:
        for i in range(KC):
            P.dma("sp", X[:, i, :], xsrc(i), writes=[Xt])
    else:
        srcA, srcB, flags = xsel
        fls = P.sbuf([128, 2], F32)
        flt = Tile()
        P.dma("sp", fls[:, :], flags[:, :], writes=[flt])
        sA = [P.sbuf([128, NC], BF16) for _ in range(2)]
        sB = [P.sbuf([128, NC], BF16) for _ in range(2)]
        sAt = [Tile(), Tile()]
        sBt = [Tile(), Tile()]
        for i in range(KC):
            q = i % 2
            P.dma("sp", sA[q][:, :], srcA(i), writes=[sAt[q]])
            P.dma("sp", sB[q][:, :], srcB(i), writes=[sBt[q]])
            P.op("dve", lambda e, q=q: e.tensor_scalar(
                out=sA[q][:, :], in0=sA[q][:, :], scalar1=fls[:, 0:1], scalar2=None, op0=ALU.mult),
                reads=[sAt[q], flt], writes=[sAt[q]])
            P.op("dve", lambda e, q=q, i=i: e.scalar_tensor_tensor(
                out=X[:, i, :], in0=sB[q][:, :], scalar=fls[:, 1:2], in1=sA[q][:, :],
                op0=ALU.mult, op1=ALU.add), reads=[sAt[q], sBt[q], flt], writes=[Xt])
    load_weight_chunk(P, wb[0], wbt[0], wr[0], KC, 128)
    P.dma("sp", rrow[0][:, :], resid[0:128, :], writes=[rt[0]])
    nsq = 0
    for j in range(16):
        s = j % 2
        if j + 1 < 16:
            load_weight_chunk(P, wb[1 - s], wbt[1 - s], wr[j + 1], KC, 128)
            P.dma("sp", rrow[1 - s][:, :], resid[(j + 1) * 128:(j + 2) * 128, :], writes=[rt[1 - s]])
        for ti, (c0, n) in enumerate(tl):
            ps, pt = banks.next()
            for i in range(KC):
                P.op("pe", lambda e, ps=ps, s=s, i=i, c0=c0, n=n: e.matmul(
                    ps[:, 0:n], wb[s][:, i * 128:(i + 1) * 128], X[:, i, c0:c0 + n],
                    start=(i == 0), stop=(i == KC - 1)), reads=[wbt[s], Xt], writes=[pt])
            P.op("dve", lambda e, ps=ps, s=s, c0=c0, n=n: e.scalar_tensor_tensor(
                out=zrow[s][:, c0:c0 + n], in0=rrow[s][:, c0:c0 + n], scalar=ALPHA, in1=ps[:, 0:n],
                op0=ALU.mult, op1=ALU.add), reads=[rt[s], pt], writes=[zt[s]])
            q = nsq % 2
            nsq += 1
            P.op("act", lambda e, s=s, q=q, c0=c0, n=n: e.activation(
                out=sq[q][:, 0:n], in_=zrow[s][:, c0:c0 + n], func=AF.Square),
                reads=[zt[s]], writes=[sqt[q]])
            P.op("dve", lambda e, s=s, c0=c0, n=n: e.tensor_tensor(
                out=s1[:, c0:c0 + n], in0=s1[:, c0:c0 + n], in1=zrow[s][:, c0:c0 + n], op=ALU.add),
                reads=[zt[s], s1t[ti]], writes=[s1t[ti]])
            P.op("dve", lambda e, q=q, c0=c0, n=n: e.tensor_tensor(
                out=s2[:, c0:c0 + n], in0=s2[:, c0:c0 + n], in1=sq[q][:, 0:n], op=ALU.add),
                reads=[sqt[q], s2t[ti]], writes=[s2t[ti]])
        P.dma("sp", Z[j * 128:(j + 1) * 128, :], zrow[s][:, :], reads=[zt[s]], writes=[Zt])
    for ti, (c0, n) in enumerate(tl):
        p1, p1t = banks.next()
        p2, p2t = banks.next()
        P.op("pe", lambda e, p1=p1, c0=c0, n=n: e.matmul(
            p1[:, 0:n], ones[:, :], s1[:, c0:c0 + n], start=True, stop=True),
            reads=[onest, s1t[ti]], writes=[p1t])
        P.op("pe", lambda e, p2=p2, c0=c0, n=n: e.matmul(
            p2[:, 0:n], ones[:, :], s2[:, c0:c0 + n], start=True, stop=True),
            reads=[onest, s2t[ti]], writes=[p2t])
        sl = slice(c0, c0 + n)
        P.op("dve", lambda e, p1=p1, sl=sl, n=n: e.tensor_scalar(
            out=mean[:, sl], in0=p1[:, 0:n], scalar1=1.0 / D, scalar2=None, op0=ALU.mult),
            reads=[p1t], writes=[stt[ti]])
        P.op("dve", lambda e, sl=sl: e.tensor_tensor(
            out=nb[:, sl], in0=mean[:, sl], in1=mean[:, sl], op=ALU.mult),
            reads=[stt[ti]], writes=[stt[ti]])
        P.op("dve", lambda e, p2=p2, sl=sl, n=n: e.scalar_tensor_tensor(
            out=rstd[:, sl], in0=p2[:, 0:n], scalar=1.0 / D, in1=nb[:, sl],
            op0=ALU.mult, op1=ALU.subtract), reads=[p2t, stt[ti]], writes=[stt[ti]])
        P.op("dve", lambda e, sl=sl: e.tensor_scalar(
            out=rstd[:, sl], in0=rstd[:, sl], scalar1=LN_EPS, scalar2=None, op0=ALU.add),
            reads=[stt[ti]], writes=[stt[ti]])
        P.op("act", lambda e, sl=sl: e.activation(out=rstd[:, sl], in_=rstd[:, sl], func=AF.Sqrt),
             reads=[stt[ti]], writes=[stt[ti]])
        P.op("dve", lambda e, sl=sl: e.reciprocal(out=rstd[:, sl], in_=rstd[:, sl]),
             reads=[stt[ti]], writes=[stt[ti]])
        P.op("dve", lambda e, sl=sl: e.scalar_tensor_tensor(
            out=nb[:, sl], in0=mean[:, sl], scalar=-1.0, in1=rstd[:, sl],
            op0=ALU.mult, op1=ALU.mult), reads=[stt[ti]], writes=[stt[ti]])
    zb = [zrow[0], zrow[1], rrow[0], rrow[1]]
    zbt = [zt[0], zt[1], rt[0], rt[1]]
    for j in range(3):
        P.dma("sp", zb[j][:, :], Z[j * 128:(j + 1) * 128, :], reads=[Zt], writes=[zbt[j]])
    for j in range(16):
        s = j % 2
        q = j % 4
        if j + 3 < 16:
            q3 = (j + 3) % 4
            P.dma("sp", zb[q3][:, :], Z[(j + 3) * 128:(j + 4) * 128, :], reads=[Zt],
                  writes=[zbt[q3]])
        P.op("dve", lambda e, q=q: e.tensor_tensor(
            out=zb[q][:, :], in0=zb[q][:, :], in1=rstd[:, :], op=ALU.mult),
            reads=[zbt[q]] + stt, writes=[zbt[q]])
        P.op("dve", lambda e, q=q: e.tensor_tensor(
            out=zb[q][:, :], in0=zb[q][:, :], in1=nb[:, :], op=ALU.add),
            reads=[zbt[q]] + stt, writes=[zbt[q]])
        P.op("act", lambda e, s=s, q=q, j=j: e.activation(
            out=orow[s][:, :], in_=zb[q][:, :], func=AF.Identity,
            scale=gbs[:, j:j + 1], bias=gbs[:, 16 + j:17 + j]),
            reads=[zbt[q], gbt], writes=[ot[s]])
        P.dma("act", out[j * 128:(j + 1) * 128, :], orow[s][:, :], reads=[ot[s]])
    P.build()


def emit_ffnup(fz, xT, wr, cw, flag, aT, WI, NH):
    P = Prog(fz)
    WO = WI - 2
    NF = DFF // 128
    X = P.sbuf([128, 16, WI], BF16)
    Xt = Tile()
    cws = P.sbuf([128, NF * 8], F32)
    cwt = Tile()
    fl = P.sbuf([128, 2], F32)
    wb = [P.sbuf([128, 16 * 256], BF16) for _ in range(2)]
    wbt = [Tile(), Tile()]
    hrow = [[P.sbuf([128, WI], F32) for _ in range(2)] for _ in range(2)]
    ht = [[Tile(), Tile()], [Tile(), Tile()]]
    crow = [P.sbuf([128, WO], F32) for _ in range(2)]
    ct = [Tile(), Tile()]
    sgrow = P.sbuf([128, WO], F32)
    sgt = Tile()
    arow = [P.sbuf([128, WO], BF16) for _ in range(2)]
    at = [Tile(), Tile()]
    banks = Banks(P, 6)

    P.dma("sp", cws[:, :], cw[:, :], writes=[cwt])
    P.dma("sp", fl[:, :], flag[:, :], writes=[cwt])
    half = WI // 2
    for i in range(16):
        for (c0, n) in ((0, half), (half, WI - half)):
            P.dma("pool", X[:, i, c0:c0 + n], xT[i * 128:(i + 1) * 128, c0:c0 + n], writes=[Xt])
    for i in range(16):
        P.op("dve", lambda e, i=i: e.tensor_scalar(
            out=X[:, i, 0:NH], in0=X[:, i, 0:NH], scalar1=fl[:, 1:2], scalar2=None, op0=ALU.mult),
            reads=[Xt, cwt], writes=[Xt])
    tl = tiles_of(WI, 411)
    load_weight_chunk(P, wb[0], wbt[0], wr[0], 16, 256)
    for f in range(NF):
        s = f % 2
        if f + 1 < NF:
            load_weight_chunk(P, wb[1 - s], wbt[1 - s], wr[f + 1], 16, 256)
        for (c0, n) in tl:
            for m in range(2):
                ps, pt = banks.next()
                for i in range(16):
                    P.op("pe", lambda e, ps=ps, s=s, i=i, m=m, c0=c0, n=n: e.matmul(
                        ps[:, 0:n], wb[s][:, i * 256 + m * 128:i * 256 + (m + 1) * 128],
                        X[:, i, c0:c0 + n], start=(i == 0), stop=(i == 15)),
                        reads=[wbt[s], Xt], writes=[pt])
                P.op("act", lambda e, ps=ps, s=s, m=m, c0=c0, n=n: e.activation(
                    out=hrow[m][s][:, c0:c0 + n], in_=ps[:, 0:n], func=AF.Copy),
                    reads=[pt], writes=[ht[m][s]])
        for m in range(2):
            b0 = f * 8 + m * 4
            P.op("dve", lambda e, s=s, m=m, b0=b0: e.tensor_scalar(
                out=crow[m][:, :], in0=hrow[m][s][:, 0:WO], scalar1=cws[:, b0:b0 + 1],
                scalar2=cws[:, b0 + 3:b0 + 4], op0=ALU.mult, op1=ALU.add),
                reads=[ht[m][s], cwt], writes=[ct[m]])
            for k in (1, 2):
                P.op("dve", lambda e, s=s, m=m, b0=b0, k=k: e.scalar_tensor_tensor(
                    out=crow[m][:, :], in0=hrow[m][s][:, k:k + WO],
                    scalar=cws[:, b0 + k:b0 + k + 1], in1=crow[m][:, :],
                    op0=ALU.mult, op1=ALU.add), reads=[ht[m][s], cwt, ct[m]], writes=[ct[m]])
        P.op("act", lambda e: e.activation(out=sgrow[:, :], in_=crow[0][:, :], func=AF.Silu),
             reads=[ct[0]], writes=[sgt])
        P.op("dve", lambda e, s=s: e.tensor_tensor(
            out=arow[s][:, :], in0=sgrow[:, :], in1=crow[1][:, :], op=ALU.mult),
            reads=[sgt, ct[1]], writes=[at[s]])
        P.dma("sp", aT[f * 128:(f + 1) * 128, :], arow[s][:, :], reads=[at[s]])
    P.build()


TWO_PI = 2.0 * math.pi
CW1 = 6.28125
CW2 = TWO_PI - CW1
MAGIC = 12582912.0
PAIRS = [[0, 1], [2, 3], [4, 5], [6, 7]]


def qkv_row(j):
    kind, jj = j // 16, j % 16
    return kind, (jj // 8) * 1024 + (jj % 8) * 128


def emit_qkv(fz, xT, wr, pos, invf, qkv_loc, G1=None):
    P = Prog(fz)
    X = P.sbuf([128, 16, TOK], BF16)
    Xt = Tile()
    wb = [P.sbuf([128, 16 * 128], BF16) for _ in range(2)]
    wbt = [Tile(), Tile()]
    posi = P.sbuf([64, TOK], I32)
    invs = P.sbuf([64, 1], F32)
    ang = P.sbuf([64, TOK], F32)
    kk = P.sbuf([64, TOK], F32)
    rr = P.sbuf([64, TOK], F32)
    Ct = P.sbuf([64, TOK], F32)
    St = P.sbuf([64, TOK], F32)
    tabt = Tile()
    E = [P.sbuf([64, TOK], F32) for _ in range(2)]
    Et = [Tile(), Tile()]
    ta = P.sbuf([64, TOK], F32)
    tb = P.sbuf([64, TOK], F32)
    tat = Tile()
    orow = [P.sbuf([128, TOK], BF16) for _ in range(2)]
    ot = [Tile(), Tile()]
    loct = [[Tile() for _ in range(4)] for _ in range(3)]
    banks = Banks(P, 6)

    half = TOK // 2
    for i in range(16):
        for (c0, n) in ((0, half), (half, half)):
            P.dma("pool", X[:, i, c0:c0 + n], xT[i * 128:(i + 1) * 128, c0:c0 + n], writes=[Xt])
    P.dma("sp", posi[:, :], pos[:, :], writes=[tabt])
    P.dma("sp", invs[:, :], invf[:, :], writes=[tabt])
    P.op("dve", lambda e: e.tensor_copy(out=ang[:, :], in_=posi[:, :]), reads=[tabt], writes=[tabt])
    P.op("dve", lambda e: e.tensor_scalar(out=ang[:, :], in0=ang[:, :], scalar1=invs[:, 0:1],
                                          scalar2=None, op0=ALU.mult), reads=[tabt], writes=[tabt])
    for (shift, dst) in ((0.0, St), (0.5 * math.pi, Ct)):
        P.op("dve", lambda e, shift=shift: e.tensor_scalar(
            out=rr[:, :], in0=ang[:, :], scalar1=shift, scalar2=None, op0=ALU.add),
            reads=[tabt], writes=[tabt])
        P.op("dve", lambda e: e.tensor_scalar(
            out=kk[:, :], in0=rr[:, :], scalar1=1.0 / TWO_PI, scalar2=MAGIC,
            op0=ALU.mult, op1=ALU.add), reads=[tabt], writes=[tabt])
        P.op("dve", lambda e: e.tensor_scalar(
            out=kk[:, :], in0=kk[:, :], scalar1=-MAGIC, scalar2=None, op0=ALU.add),
            reads=[tabt], writes=[tabt])
        P.op("dve", lambda e: e.scalar_tensor_tensor(
            out=rr[:, :], in0=kk[:, :], scalar=-CW1, in1=rr[:, :], op0=ALU.mult, op1=ALU.add),
            reads=[tabt], writes=[tabt])
        P.op("dve", lambda e: e.scalar_tensor_tensor(
            out=rr[:, :], in0=kk[:, :], scalar=-CW2, in1=rr[:, :], op0=ALU.mult, op1=ALU.add),
            reads=[tabt], writes=[tabt])
        P.op("dve", lambda e: e.tensor_scalar(
            out=rr[:, :], in0=rr[:, :], scalar1=math.pi, scalar2=-math.pi,
            op0=ALU.min, op1=ALU.max), reads=[tabt], writes=[tabt])
        P.op("act", lambda e, dst=dst: e.activation(out=dst[:, :], in_=rr[:, :], func=AF.Sin),
             reads=[tabt], writes=[tabt])
    tl = tiles_of(TOK, 512)
    load_weight_chunk(P, wb[0], wbt[0], wr[0], 16, 128)
    for j in range(48):
        s = j % 2
        if j + 1 < 48:
            load_weight_chunk(P, wb[1 - s], wbt[1 - s], wr[j + 1], 16, 128)
        rope = j < 32
        for (c0, n) in tl:
            ps, pt = banks.next()
            for i in range(16):
                P.op("pe", lambda e, ps=ps, s=s, i=i, c0=c0, n=n: e.matmul(
                    ps[:, 0:n], wb[s][:, i * 128:(i + 1) * 128], X[:, i, c0:c0 + n],
                    start=(i == 0), stop=(i == 15)), reads=[wbt[s], Xt], writes=[pt])
            if rope:
                P.op("act", lambda e, ps=ps, s=s, c0=c0, n=n: e.activation(
                    out=E[s][0:64, c0:c0 + n], in_=ps[0:64, 0:n], func=AF.Copy),
                    reads=[pt], writes=[Et[s]])
                P.op("act", lambda e, ps=ps, s=s, c0=c0, n=n: e.activation(
                    out=orow[s][64:128, c0:c0 + n], in_=ps[64:128, 0:n], func=AF.Copy),
                    reads=[pt], writes=[ot[s]])
            else:
                P.op("act", lambda e, ps=ps, s=s, c0=c0, n=n: e.activation(
                    out=orow[s][:, c0:c0 + n], in_=ps[:, 0:n], func=AF.Copy),
                    reads=[pt], writes=[ot[s]])
        if rope:
            P.op("dve", lambda e, s=s: e.tensor_tensor(
                out=ta[:, :], in0=E[s][:, :], in1=Ct[:, :], op=ALU.mult),
                reads=[Et[s], tabt], writes=[tat])
            P.op("dve", lambda e, s=s: e.tensor_tensor(
                out=tb[0:32, :], in0=E[s][32:64, :], in1=St[32:64, :], op=ALU.mult),
                reads=[Et[s], tabt], writes=[tat])
            P.op("dve", lambda e, s=s: e.tensor_tensor(
                out=tb[32:64, :], in0=E[s][0:32, :], in1=St[0:32, :], op=ALU.mult),
                reads=[Et[s], tabt], writes=[tat])
            P.op("dve", lambda e, s=s: e.tensor_tensor(
                out=orow[s][0:32, :], in0=ta[0:32, :], in1=tb[0:32, :], op=ALU.subtract),
                reads=[tat], writes=[ot[s]])
            P.op("dve", lambda e, s=s: e.tensor_tensor(
                out=orow[s][32:64, :], in0=ta[32:64, :], in1=tb[32:64, :], op=ALU.add),
                reads=[tat], writes=[ot[s]])
        kind, r0 = qkv_row(j)
        ck, rc = r0 // 512, r0 % 512
        P.dma("sp", qkv_loc[kind][ck][rc:rc + 128, :], orow[s][:, :], reads=[ot[s]],
              writes=[loct[kind][ck]])
        if rc == 384:
            P.coll("AllGather", PAIRS, qkv_loc[kind][ck], G1[kind][ck], reads=[loct[kind][ck]],
                   writes=[Tile()])
    P.build()


NH_CORE = 4
QT = 512


def emit_attn(fz, G1, flags, lam, sg, tri, ident, o_loc, G2):
    P = Prog(fz)
    qh = [P.sbuf([128, 2, S], BF16) for _ in range(2)]
    kh = [P.sbuf([128, 2, S], BF16) for _ in range(2)]
    vT = P.sbuf([128, 2, S], BF16)
    vh = [P.sbuf([128, 32 * 256], BF16) for _ in range(2)]
    stA = P.sbuf([128, 2, S], BF16)
    stB = P.sbuf([128, 2, S], BF16)
    stAt, stBt = Tile(), Tile()
    fls = P.sbuf([128, 2], F32)
    qt_ = [Tile(), Tile()]
    kt_ = [Tile(), Tile()]
    vTt = Tile()
    vt_ = [Tile(), Tile()]
    lams = P.sbuf([128, 4], F32)
    sgs = P.sbuf([128, 2], F32)
    tris = P.sbuf([128, 128], BF16)
    idn = P.sbuf([128, 128], BF16)
    zer = P.sbuf([128, 2], BF16)
    cst = Tile()
    ones_b = P.sbuf([128, 128], BF16)
    ones_f = P.sbuf([128, 128], F32)
    prod = P.sbuf([128, 2], F32)
    ee = P.sbuf([128, 2], F32)
    nlam = P.sbuf([128, 1], F32)
    sgl = P.sbuf([128, 2], F32)
    NPT = 6
    pt_sb = [P.sbuf([128, QT], BF16) for _ in range(NPT)]
    ptt = [Tile() for _ in range(NPT)]
    rec = P.sbuf([128, QT], F32)
    rect = Tile()
    on0 = [P.sbuf([128, QT], F32) for _ in range(2)]
    on0t = [Tile(), Tile()]
    oo = [P.sbuf([128, QT], F32) for _ in range(2)]
    oot = [Tile(), Tile()]
    t1 = P.sbuf([128, QT], F32)
    t1t = Tile()
    osq = [P.sbuf([128, QT], F32) for _ in range(2)]
    osqt = [Tile(), Tile()]
    rms = P.sbuf([128, QT], F32)
    rmst = Tile()
    ob = [[P.sbuf([128, QT], BF16) for _ in range(2)] for _ in range(2)]
    obt = [[Tile(), Tile()], [Tile(), Tile()]]
    sbanks = Banks(P, 4)
    pO = [P.psum(), P.psum()]
    pOt = [Tile(), Tile()]
    pL = P.psum()
    pLt = Tile()
    pX, pXt = sbanks.b[3], sbanks.t[3]
    pT = [P.psum([128, 512], BF16)]
    pTt = [Tile()]
    g1t = Tile()
    olt = [Tile() for _ in range(8)]

    def fetch(t, h):
        for c in range(2):
            L = (h * 2 + c) * 128
            sub, rc = L // 512, L % 512
            for r in range(2):
                rows = slice(r * 512 + rc, r * 512 + rc + 128)
                P.dma("sp", stA[:, c, r * TOK:(r + 1) * TOK], G1[t][sub][rows, :],
                      reads=[g1t], writes=[stAt])
                P.dma("sp", stB[:, c, r * TOK:(r + 1) * TOK], G1[t][2 + sub][rows, :],
                      reads=[g1t], writes=[stBt])

    def select(dst, dstt):
        P.op("dve", lambda e: e.tensor_scalar(
            out=stA[:, :, :], in0=stA[:, :, :], scalar1=fls[:, 0:1], scalar2=None, op0=ALU.mult),
            reads=[stAt, cst], writes=[stAt])
        P.op("dve", lambda e: e.scalar_tensor_tensor(
            out=dst[:, :, :], in0=stB[:, :, :], scalar=fls[:, 1:2], in1=stA[:, :, :],
            op0=ALU.mult, op1=ALU.add), reads=[stAt, stBt, cst], writes=[dstt])

    def load_v(h, s):
        for kp in range(16):
            b = 0
            for u in range(4):
                kc, hf = kp * 2 + u // 2, u % 2
                P.op("pe", lambda e, b=b, u=u, kc=kc, hf=hf: e.transpose(
                    pT[b][:, u * 128:(u + 1) * 128], vT[:, hf, kc * 128:(kc + 1) * 128], idn[:, :]),
                    reads=[vTt, cst], writes=[pTt[b]])
            P.op("act", lambda e, b=b, kp=kp, s=s: e.activation(
                out=vh[s][:, kp * 512:(kp + 1) * 512], in_=pT[b][:, :], func=AF.Copy),
                reads=[pTt[b]], writes=[vt_[s]])

    P.dma("sp", fls[:, :], flags[:, :], writes=[cst])
    P.dma("sp", lams[:, :], lam[:, :], writes=[cst])
    P.dma("sp", sgs[:, :], sg[:, :], writes=[cst])
    P.dma("sp", tris[:, :], tri[:, :], writes=[cst])
    P.dma("sp", idn[:, :], ident[:, :], writes=[cst])
    P.op("dve", lambda e: e.memset(zer[:, :], 0.0), writes=[cst])
    P.op("dve", lambda e: e.memset(ones_b[:, :], 1.0), writes=[cst])
    P.op("dve", lambda e: e.memset(ones_f[:, :], 1.0), writes=[cst])
    fetch(0, 0)
    select(qh[0], qt_[0])
    fetch(1, 0)
    select(kh[0], kt_[0])
    fetch(2, 0)
    select(vT, vTt)
    load_v(0, 0)
    for hh in range(8):
        P.dma("sp", o_loc[hh // 2][(hh % 2) * 128:(hh % 2 + 1) * 128, 0:2], zer[:, :],
              reads=[cst], writes=[olt[hh // 2]])
    P.op("dve", lambda e: e.tensor_tensor(out=prod[:, 0:1], in0=lams[:, 0:1], in1=lams[:, 1:2],
                                          op=ALU.mult), reads=[cst], writes=[cst])
    P.op("dve", lambda e: e.tensor_tensor(out=prod[:, 1:2], in0=lams[:, 2:3], in1=lams[:, 3:4],
                                          op=ALU.mult), reads=[cst], writes=[cst])
    P.op("pe", lambda e: e.matmul(pX[:, 0:2], ones_f[:, :], prod[:, 0:2], start=True, stop=True),
         reads=[cst], writes=[pXt])
    P.op("act", lambda e: e.activation(out=ee[:, :], in_=pX[:, 0:2], func=AF.Exp),
         reads=[pXt], writes=[cst])
    P.op("dve", lambda e: e.tensor_tensor(out=nlam[:, :], in0=ee[:, 1:2], in1=ee[:, 0:1],
                                          op=ALU.subtract), reads=[cst], writes=[cst])
    P.op("dve", lambda e: e.tensor_scalar(out=nlam[:, :], in0=nlam[:, :], scalar1=-LAMBDA_INIT,
                                          scalar2=None, op0=ALU.add), reads=[cst], writes=[cst])
    P.op("dve", lambda e: e.tensor_scalar(out=sgl[:, :], in0=sgs[:, :], scalar1=1.0 - LAMBDA_INIT,
                                          scalar2=None, op0=ALU.mult), reads=[cst], writes=[cst])
    scale = 128.0 ** -0.5
    oraw = [[P.sbuf([128, QT], F32) for _ in range(2)] for _ in range(2)]
    orawt = [[Tile(), Tile()], [Tile(), Tile()]]
    lraw = [P.sbuf([128, QT], F32) for _ in range(2)]
    lrawt = [Tile(), Tile()]

    LAG = 3
    blocks = []
    for h in range(NH_CORE):
        for qi in range(S // QT):
            for c in range(2):
                nkc = 4 * (qi + 1)
                for kc in range(nkc):
                    jl = kc - 4 * qi
                    c0 = 128 * jl if jl > 0 else 0
                    blocks.append((h, qi, c, kc, jl, c0, QT - c0, kc == 0, kc == nkc - 1))
    state = {"npt": 0, "nob": 0, "ng": 0}
    pinfo = {}

    def front(i):
        h, qi, c, kc, jl, c0, n, first, last = blocks[i]
        s = h % 2
        q0 = qi * QT
        nxt = h + 1 < NH_CORE
        if first and c == 0:
            if nxt and qi == 0:
                fetch(0, h + 1)
            if nxt and qi == 2:
                select(qh[1 - s], qt_[1 - s])
                fetch(1, h + 1)
            if nxt and qi == 4:
                select(kh[1 - s], kt_[1 - s])
                fetch(2, h + 1)
            if nxt and qi == 6:
                select(vT, vTt)
            if nxt and qi == 7:
                load_v(h + 1, 1 - s)
        ps, pst = sbanks.next()
        P.op("pe", lambda e, ps=ps, s=s, c=c, kc=kc, c0=c0, n=n, q0=q0: e.matmul(
            ps[:, c0:c0 + n], kh[s][:, c, kc * 128:(kc + 1) * 128],
            qh[s][:, c, q0 + c0:q0 + c0 + n], start=True, stop=True),
            reads=[kt_[s], qt_[s]], writes=[pst])
        pi = state["npt"] % NPT
        state["npt"] += 1
        pinfo[i] = pi
        P.op("act", lambda e, ps=ps, pi=pi, c0=c0, n=n: e.activation(
            out=pt_sb[pi][:, c0:c0 + n], in_=ps[:, c0:c0 + n], func=AF.Exp, scale=scale),
            reads=[pst], writes=[ptt[pi]])
        if jl >= 0:
            P.op("pool", lambda e, pi=pi, c0=c0: e.tensor_tensor(
                out=pt_sb[pi][:, c0:c0 + 128], in0=pt_sb[pi][:, c0:c0 + 128],
                in1=tris[:, :], op=ALU.mult), reads=[ptt[pi], cst], writes=[ptt[pi]])

    def back(i):
        h, qi, c, kc, jl, c0, n, first, last = blocks[i]
        s = h % 2
        q0 = qi * QT
        pi = pinfo.pop(i)
        for hf in range(2):
            P.op("pe", lambda e, hf=hf, s=s, kc=kc, pi=pi, c0=c0, n=n, first=first,
                 last=last: e.matmul(
                pO[hf][:, c0:c0 + n],
                vh[s][:, kc * 256 + hf * 128:kc * 256 + (hf + 1) * 128],
                pt_sb[pi][:, c0:c0 + n], start=first, stop=last),
                reads=[vt_[s], ptt[pi]], writes=[pOt[hf]])
        P.op("pe", lambda e, pi=pi, c0=c0, n=n, first=first, last=last: e.matmul(
            pL[:, c0:c0 + n], ones_b[:, :], pt_sb[pi][:, c0:c0 + n],
            start=first, stop=last), reads=[cst, ptt[pi]], writes=[pLt])
        if not last:
            return
        g = state["ng"] % 2
        state["ng"] += 1
        P.op("act", lambda e, g=g: e.activation(out=lraw[g][:, :], in_=pL[:, :], func=AF.Copy),
             reads=[pLt], writes=[lrawt[g]])
        for hf in range(2):
            P.op("act", lambda e, g=g, hf=hf: e.activation(
                out=oraw[g][hf][:, :], in_=pO[hf][:, :], func=AF.Copy),
                reads=[pOt[hf]], writes=[orawt[g][hf]])
        P.op("dve", lambda e, g=g: e.reciprocal(out=rec[:, :], in_=lraw[g][:, :]),
             reads=[lrawt[g]], writes=[rect])
        for hf in range(2):
            if c == 0:
                P.op("dve", lambda e, hf=hf, g=g: e.tensor_tensor(
                    out=on0[hf][:, :], in0=oraw[g][hf][:, :], in1=rec[:, :], op=ALU.mult),
                    reads=[orawt[g][hf], rect], writes=[on0t[hf]])
            else:
                P.op("dve", lambda e, hf=hf, g=g: e.tensor_tensor(
                    out=t1[:, :], in0=oraw[g][hf][:, :], in1=rec[:, :], op=ALU.mult),
                    reads=[orawt[g][hf], rect], writes=[t1t])
                P.op("dve", lambda e, hf=hf: e.scalar_tensor_tensor(
                    out=oo[hf][:, :], in0=t1[:, :], scalar=nlam[:, 0:1], in1=on0[hf][:, :],
                    op0=ALU.mult, op1=ALU.add), reads=[t1t, on0t[hf], cst], writes=[oot[hf]])
        if c == 0:
            return
        for hf in range(2):
            P.op("act", lambda e, hf=hf: e.activation(out=osq[hf][:, :], in_=oo[hf][:, :],
                                                      func=AF.Square),
                 reads=[oot[hf]], writes=[osqt[hf]])
            P.op("pe", lambda e, hf=hf: e.matmul(pX[:, :], ones_f[:, :], osq[hf][:, :],
                                                  start=(hf == 0), stop=(hf == 1)),
                 reads=[cst, osqt[hf]], writes=[pXt])
        P.op("dve", lambda e: e.tensor_scalar(out=rms[:, :], in0=pX[:, :], scalar1=1.0 / 256.0,
                                              scalar2=LN_EPS, op0=ALU.mult, op1=ALU.add),
             reads=[pXt], writes=[rmst])
        P.op("act", lambda e: e.activation(out=rms[:, :], in_=rms[:, :], func=AF.Sqrt),
             reads=[rmst], writes=[rmst])
        P.op("dve", lambda e: e.reciprocal(out=rms[:, :], in_=rms[:, :]),
             reads=[rmst], writes=[rmst])
        b = state["nob"] % 2
        state["nob"] += 1
        jh = q0 // TOK
        col = q0 - jh * TOK + 2
        for hf in range(2):
            P.op("dve", lambda e, hf=hf, b=b: e.scalar_tensor_tensor(
                out=ob[b][hf][:, :], in0=oo[hf][:, :], scalar=sgl[:, hf:hf + 1], in1=rms[:, :],
                op0=ALU.mult, op1=ALU.mult), reads=[oot[hf], rmst, cst], writes=[obt[b][hf]])
            ck = jh * 4 + h
            P.dma("sp", o_loc[ck][hf * 128:(hf + 1) * 128, col:col + QT], ob[b][hf][:, :],
                  reads=[obt[b][hf]], writes=[olt[ck]])
            if q0 + QT == TOK:
                P.dma("sp", o_loc[4 + h][hf * 128:(hf + 1) * 128, 0:2],
                      ob[b][hf][:, QT - 2:QT], reads=[obt[b][hf]], writes=[olt[4 + h]])
        if (q0 + QT) % TOK == 0:
            ck = jh * 4 + h
            P.coll("AllGather", PAIRS, o_loc[ck], G2[ck], reads=[olt[ck]], writes=[Tile()])

    for i in range(len(blocks) + LAG):
        if i < len(blocks):
            front(i)
        if i >= LAG:
            back(i - LAG)
    P.build()


H0 = 6


def build_fused():
    fz = Fused()
    NF = DFF // 128
    ei = lambda n, sh, dt: fz.dram(n, sh, dt, "ExternalInput")
    xT = ei("xT", [D, TOK + H0], F32)
    w_in = ei("w_in", [16, 128, 16 * 384], F32)
    cwm = ei("cwm", [128, 48], F32)
    w_out = ei("w_out", [16, 128, 16 * 128], F32)
    gb = [ei("gb%d" % i, [128, 32], F32) for i in range(4)]
    w_up = [ei("w_up%d" % i, [NF, 128, 16 * 256], F32) for i in range(2)]
    cwf = [ei("cwf%d" % i, [128, NF * 8], F32) for i in range(2)]
    w_dn = [ei("w_dn%d" % i, [16, 128, NF * 128], F32) for i in range(2)]
    flag = ei("flag", [128, 2], F32)
    w_qkv = ei("w_qkv", [48, 128, 16 * 128], F32)
    pos = ei("pos", [64, TOK], I32)
    invf = ei("invf", [64, 1], F32)
    lam = ei("lam", [128, 4], F32)
    sg = ei("sg", [128, 2], F32)
    tri = ei("tri", [128, 128], BF16)
    ident = ei("ident", [128, 128], BF16)
    w_o = ei("w_o", [16, 128, 16 * 128], F32)
    outT = fz.dram("outT", [D, TOK], F32, "ExternalOutput")

    yT = fz.dram("yT", [D, TOK + 4], BF16)
    Z = fz.dram("Z", [D, TOK + 4], F32)
    x1 = fz.dram("x1", [D, TOK + 4], F32)
    aT = fz.dram("aT", [DFF, TOK + 2], BF16)
    x2 = fz.dram("x2", [D, TOK + 2], F32)
    qkv_loc = [[fz.dram("ql%d_%d" % (t, k), [512, TOK], BF16) for k in range(4)] for t in range(3)]
    G1 = [[fz.dram("qg%d_%d" % (t, k), [1024, TOK], BF16) for k in range(4)] for t in range(3)]
    o_loc = [fz.dram("ol%d" % k, [256, TOK + 2], BF16) for k in range(8)]
    G2 = [fz.dram("og%d" % k, [512, TOK + 2], BF16) for k in range(8)]
    x3 = fz.dram("x3", [D, TOK + 2], F32)

    emit_mixin(fz, xT, w_in, cwm, yT, TOK + H0)
    NC = TOK + 4
    emit_pln(fz, lambda i: yT[i * 128:(i + 1) * 128, :], w_out, xT[:, 2:2 + NC], gb[0],
             Z[:, 0:NC], x1[:, :], 16, NC, 411)
    emit_ffnup(fz, x1, w_up[0], cwf[0], flag, aT, TOK + 4, 4)
    hw = (TOK + 2) // 2
    for c0 in (0, hw):
        emit_pln(fz, lambda i, c0=c0: aT[i * 128:(i + 1) * 128, c0:c0 + hw], w_dn[0],
                 x1[:, 2 + c0:2 + c0 + hw], gb[1], Z[:, 0:hw], x2[:, c0:c0 + hw], NF, hw, 342)
    emit_qkv(fz, x2[:, 2:2 + TOK], w_qkv, pos, invf, qkv_loc, G1)
    emit_attn(fz, G1, flag, lam, sg, tri, ident, o_loc, G2)
    NC = TOK + 2

    def osrc(i, half):
        r, L = i // 8, (i % 8) * 128
        return G2[half * 4 + L // 256][r * 256 + L % 256:r * 256 + L % 256 + 128, :]

    emit_pln(fz, None, w_o, x2[:, 0:NC], gb[2], Z[:, 0:NC], x3[:, :], 16, NC, 410,
             xsel=(lambda i: osrc(i, 0), lambda i: osrc(i, 1), flag))
    aT1 = aT[:, 0:TOK]
    emit_ffnup(fz, x3, w_up[1], cwf[1], flag, aT1, TOK + 2, 2)
    for c0 in (0, TOK // 2):
        emit_pln(fz, lambda i, c0=c0: aT[i * 128:(i + 1) * 128, c0:c0 + TOK // 2], w_dn[1],
                 x3[:, 2 + c0:2 + c0 + TOK // 2], gb[3], Z[:, 0:TOK // 2],
                 outT[:, c0:c0 + TOK // 2], NF, TOK // 2, 512)
    return fz.finish()


_cache = {}


def arrange_w(w):
    K, F = w.shape
    kc = K // 128
    a = w.reshape(kc, 128, F // 128, 128).transpose(2, 1, 0, 3)
    return np.ascontiguousarray(a).reshape(F // 128, 128, kc * 128)


def pvec(v):
    return np.ascontiguousarray(v.reshape(-1, 128).T)


def ffn_layout(w_up, conv_w, conv_b):
    NF = DFF // 128
    wg = w_up[:, :DFF].reshape(16, 128, NF, 128)
    wu = w_up[:, DFF:].reshape(16, 128, NF, 128)
    wr = np.stack([wg, wu], axis=3)
    wr = np.ascontiguousarray(wr.transpose(2, 1, 0, 3, 4)).reshape(NF, 128, 16 * 256)
    cw = np.zeros((128, NF, 2, 4), np.float32)
    for m in range(2):
        sl = slice(m * DFF, (m + 1) * DFF)
        for k in range(3):
            cw[:, :, m, k] = pvec(conv_w[k, sl])
        cw[:, :, m, 3] = pvec(conv_b[sl])
    return wr, cw.reshape(128, NF * 8)


ROPE_PERM = np.concatenate([np.arange(0, 16), np.arange(32, 48), np.arange(16, 32),
                            np.arange(48, 128)])


def kernel(x, positions, ln_g, ln_b, a_w_in, a_conv_w, a_w_out, kv_w_k, kv_w_v,
           b_w_q, b_lambda, b_subln_g, b_w_o, ffn_w_up, ffn_conv_w, ffn_conv_b, ffn_w_down):
    f = lambda a: np.asarray(a, dtype=np.float32)
    x = f(x)
    xTf = x.reshape(B * S, D).T
    shared = {}
    w_in = f(a_w_in[0])
    wi = np.stack([w_in[:, m * D:(m + 1) * D].reshape(16, 128, 16, 128) for m in range(3)], axis=3)
    shared["w_in"] = np.ascontiguousarray(wi.transpose(2, 1, 0, 3, 4)).reshape(16, 128, 16 * 384)
    shared["cwm"] = np.stack([pvec(f(a_conv_w[0, k])) for k in range(3)], axis=2).reshape(128, 48)
    shared["w_out"] = arrange_w(f(a_w_out[0]))
    for i, (l, k) in enumerate(((0, 0), (0, 1), (1, 0), (1, 1))):
        shared["gb%d" % i] = np.concatenate([pvec(f(ln_g[l, k])), pvec(f(ln_b[l, k]))], axis=1)
    for l in range(2):
        wr, cw = ffn_layout(f(ffn_w_up[l]), f(ffn_conv_w[l]), f(ffn_conv_b[l]))
        shared["w_up%d" % l] = wr
        shared["cwf%d" % l] = cw
        shared["w_dn%d" % l] = arrange_w(f(ffn_w_down[l]))
    perm = (np.arange(16)[:, None] * 128 + ROPE_PERM[None, :]).reshape(-1)
    shared["w_qkv"] = arrange_w(np.concatenate(
        [f(b_w_q[0])[:, perm], f(kv_w_k)[:, perm], f(kv_w_v)], axis=1))
    invf = np.zeros((64, 1), np.float32)
    fr = (ROPE_THETA ** (-np.arange(0, 32, 2, dtype=np.float32) / 32.0)).astype(np.float32)
    invf[0:16, 0] = fr
    invf[32:48, 0] = fr
    shared["invf"] = invf
    shared["lam"] = np.ascontiguousarray(f(b_lambda[0]).T)
    shared["sg"] = pvec(f(b_subln_g[0]))
    shared["tri"] = (np.arange(128)[:, None] <= np.arange(128)[None, :]).astype(NPBF)
    shared["ident"] = np.eye(128).astype(NPBF)
    shared["w_o"] = arrange_w(f(b_w_o[0]))
    posf = np.asarray(positions, dtype=np.int32).reshape(B * S)
    maps = []
    for c in range(NCORE):
        m = dict(shared)
        a = np.zeros((D, TOK + H0), np.float32)
        a[:, H0:] = xTf[:, c * TOK:(c + 1) * TOK]
        if c % 2 == 1:
            a[:, 0:H0] = xTf[:, c * TOK - H0:c * TOK]
        m["xT"] = a
        m["flag"] = np.tile(np.array([[1.0 - c % 2, float(c % 2)]], np.float32), (128, 1))
        m["pos"] = np.ascontiguousarray(np.broadcast_to(posf[c * TOK:(c + 1) * TOK][None, :],
                                                        (64, TOK)))
        maps.append(m)
    if "nc" not in _cache:
        _cache["nc"] = build_fused()
    res = run_bass_kernel_spmd(_cache["nc"], maps, core_ids=list(range(NCORE))).results
    outT = np.concatenate([r["outT"] for r in res], axis=1)
    return np.ascontiguousarray(outT.T).reshape(B, S, D).astype(np.float32)
```
